# Optimizing a Trainium2 kernel written in Bass

```python
import math
import jax, jax.numpy as jnp
from jax import lax
import numpy as np

D_MODEL = 2048
BATCH = 4
SEQ = 8192
DEPTH = 2
DEC_BATCH = 8
DEC_SEQ = 16
PAST_LEN = 2048

CHUNK = 64
Q_BLOCK = 128
N_MIXERS = 2
SB_HEADS = 16
SB_HEAD_DIM = D_MODEL // SB_HEADS
DIFF_HEADS = 8
DIFF_QK_DIM = D_MODEL // (2 * DIFF_HEADS)
DIFF_V_DIM = 2 * DIFF_QK_DIM
D_FF = 4 * D_MODEL
N_DIFF_LAYERS = DEPTH // N_MIXERS
NORM_EPS = 1e-6
SUBLN_EPS = 1e-5

kernel_name = "sb_diff_alibi_stream_step"


def rmsnorm(x, g, eps=NORM_EPS):
    xf = x.astype(jnp.float32)
    y = xf * lax.rsqrt(jnp.mean(xf * xf, axis=-1, keepdims=True) + eps)
    return (y * g.astype(jnp.float32)).astype(x.dtype)


def alibi_slopes(n):
    return jnp.asarray(np.array([2.0 ** (-8.0 * (i + 1) / n) for i in range(n)], dtype=np.float32))


def diff_lambda_init(layer_idx):
    return 0.8 - 0.6 * math.exp(-0.3 * layer_idx)


def stick_breaking_attn(q, k, v, q_pos, k_pos):
    z = jnp.einsum("bqhd,bkhd->bhqk", q.astype(jnp.float32), k.astype(jnp.float32)) * (SB_HEAD_DIM ** -0.5)
    allowed = k_pos[None, :] < q_pos[:, None]
    log_fail = jnp.where(allowed, jax.nn.log_sigmoid(-z), 0.0)
    suffix = lax.cumsum(log_fail, axis=3, reverse=True) - log_fail
    w = jnp.where(allowed, jnp.exp(jax.nn.log_sigmoid(z) + suffix), 0.0)
    out = jnp.einsum("bhqk,bkhd->bqhd", w, v.astype(jnp.float32))
    return out.astype(q.dtype)


def diff_attn(q, k, v, q_pos, k_pos, lam, subln_g, lambda_init):
    B, Tq = q.shape[:2]
    Tk = k.shape[1]
    s = jnp.einsum("bqhd,bkhd->bhqk", q.astype(jnp.float32), k.astype(jnp.float32)) * (DIFF_QK_DIM ** -0.5)
    slopes = jnp.repeat(alibi_slopes(DIFF_HEADS), 2)
    dist = jnp.abs(q_pos[:, None] - k_pos[None, :]).astype(jnp.float32)
    allowed = (k_pos[None, :] // CHUNK) <= (q_pos[:, None] // CHUNK)
    s = jnp.where(allowed, s - slopes[:, None, None] * dist, -jnp.inf)
    p = jax.nn.softmax(s, axis=-1).reshape(B, DIFF_HEADS, 2, Tq, Tk)
    w = p[:, :, 0] - lam * p[:, :, 1]
    o = jnp.einsum("bhqk,bkhd->bqhd", w, v.astype(jnp.float32))
    o = rmsnorm(o, subln_g, SUBLN_EPS) * (1.0 - lambda_init)
    return o.astype(q.dtype)


def sweep_query_blocks(attn_fn, q, q_pos):
    B, T = q.shape[:2]
    if T <= Q_BLOCK:
        return attn_fn(q, q_pos)
    nb = T // Q_BLOCK
    qb = jnp.moveaxis(q.reshape(B, nb, Q_BLOCK, *q.shape[2:]), 1, 0)
    pb = q_pos.reshape(nb, Q_BLOCK)
    ob = lax.map(lambda a: attn_fn(a[0], a[1]), (qb, pb))
    ob = jnp.moveaxis(ob, 0, 1)
    return ob.reshape(B, T, *ob.shape[3:])


def trunk_layer(i, x, past_k, past_v, q_pos, k_pos, g_attn, w_qkv_i, w_o_i,
                lq1, lk1, lq2, lk2, subln_g, g_mlp, w_up_i, w_down_i):
    B, T, _ = x.shape
    h = rmsnorm(x, g_attn)
    q, k, v = jnp.split(h @ w_qkv_i, 3, axis=-1)
    k_all = k if past_k is None else jnp.concatenate([past_k.astype(k.dtype), k], axis=1)
    v_all = v if past_v is None else jnp.concatenate([past_v.astype(v.dtype), v], axis=1)
    Tk = k_all.shape[1]
    if i % N_MIXERS == 0:
        qh = q.reshape(B, T, SB_HEADS, SB_HEAD_DIM)
        kh = k_all.reshape(B, Tk, SB_HEADS, SB_HEAD_DIM)
        vh = v_all.reshape(B, Tk, SB_HEADS, SB_HEAD_DIM)
        fn = lambda qb, pb: stick_breaking_attn(qb, kh, vh, pb, k_pos)
    else:
        j = i // N_MIXERS
        lam = (jnp.exp(jnp.sum(lq1[j].astype(jnp.float32) * lk1[j].astype(jnp.float32)))
               - jnp.exp(jnp.sum(lq2[j].astype(jnp.float32) * lk2[j].astype(jnp.float32)))
               + diff_lambda_init(i))
        qh = q.reshape(B, T, 2 * DIFF_HEADS, DIFF_QK_DIM)
        kh = k_all.reshape(B, Tk, 2 * DIFF_HEADS, DIFF_QK_DIM)
        vh = v_all.reshape(B, Tk, DIFF_HEADS, DIFF_V_DIM)
        g_sub = subln_g[j]
        li = diff_lambda_init(i)
        fn = lambda qb, pb: diff_attn(qb, kh, vh, pb, k_pos, lam, g_sub, li)
    mix = sweep_query_blocks(fn, qh, q_pos).reshape(B, T, D_MODEL)
    x = x + mix @ w_o_i
    h2 = rmsnorm(x, g_mlp)
    x = x + jnp.square(jax.nn.relu(h2 @ w_up_i)) @ w_down_i
    return x, k, v


def setup_inputs(seed: int = 0) -> dict:
    key = jax.random.key(seed)
    ks = jax.random.split(key, 16)
    f32 = jnp.float32
    nd = max(N_DIFF_LAYERS, 1) if DEPTH > 1 else N_DIFF_LAYERS
    return {
        "x_prompt": jax.random.normal(ks[0], (BATCH, SEQ, D_MODEL), f32),
        "x_sample": jax.random.normal(ks[1], (DEC_BATCH, DEC_SEQ, D_MODEL), f32),
        "cache_k": jax.random.normal(ks[2], (DEPTH, DEC_BATCH, PAST_LEN, D_MODEL), f32),
        "cache_v": jax.random.normal(ks[3], (DEPTH, DEC_BATCH, PAST_LEN, D_MODEL), f32),
        "norm_attn": 1.0 + 0.01 * jax.random.normal(ks[4], (DEPTH, D_MODEL), f32),
        "w_qkv": jax.random.normal(ks[5], (DEPTH, D_MODEL, 3 * D_MODEL), f32) * D_MODEL ** -0.5,
        "w_o": jax.random.normal(ks[6], (DEPTH, D_MODEL, D_MODEL), f32) * D_MODEL ** -0.5,
        "lambda_q1": 0.1 * jax.random.normal(ks[7], (nd, DIFF_QK_DIM), f32),
        "lambda_k1": 0.1 * jax.random.normal(ks[8], (nd, DIFF_QK_DIM), f32),
        "lambda_q2": 0.1 * jax.random.normal(ks[9], (nd, DIFF_QK_DIM), f32),
        "lambda_k2": 0.1 * jax.random.normal(ks[10], (nd, DIFF_QK_DIM), f32),
        "subln_g": 1.0 + 0.01 * jax.random.normal(ks[11], (nd, DIFF_V_DIM), f32),
        "norm_mlp": 1.0 + 0.01 * jax.random.normal(ks[12], (DEPTH, D_MODEL), f32),
        "w_up": jax.random.normal(ks[13], (DEPTH, D_MODEL, D_FF), f32) * D_MODEL ** -0.5,
        "w_down": jax.random.normal(ks[14], (DEPTH, D_FF, D_MODEL), f32) * D_FF ** -0.5,
        "norm_final": 1.0 + 0.01 * jax.random.normal(ks[15], (D_MODEL,), f32),
    }


def reference(x_prompt, x_sample, cache_k, cache_v, norm_attn, w_qkv, w_o,
              lambda_q1, lambda_k1, lambda_q2, lambda_k2, subln_g,
              norm_mlp, w_up, w_down, norm_final):
    t_p = x_prompt.shape[1]
    t_s = x_sample.shape[1]
    past = cache_k.shape[2]
    pos_p = jnp.arange(t_p, dtype=jnp.int32)
    pos_s = past + jnp.arange(t_s, dtype=jnp.int32)
    pos_s_keys = jnp.arange(past + t_s, dtype=jnp.int32)
    xp, xs = x_prompt, x_sample
    kp, vp, ksm, vsm = [], [], [], []
    for i in range(DEPTH):
        shared = (norm_attn[i], w_qkv[i], w_o[i], lambda_q1, lambda_k1, lambda_q2, lambda_k2,
                  subln_g, norm_mlp[i], w_up[i], w_down[i])
        xp, k_new, v_new = trunk_layer(i, xp, None, None, pos_p, pos_p, *shared)
        xs, k_s, v_s = trunk_layer(i, xs, cache_k[i], cache_v[i], pos_s, pos_s_keys, *shared)
        kp.append(k_new); vp.append(v_new); ksm.append(k_s); vsm.append(v_s)
    y_prompt = rmsnorm(xp, norm_final)
    y_sample = rmsnorm(xs, norm_final)
    return (y_prompt, y_sample, jnp.stack(kp), jnp.stack(vp), jnp.stack(ksm), jnp.stack(vsm))
```

```python
import math
from contextlib import ExitStack

import numpy as np
import ml_dtypes

import concourse.bass as bass
import concourse.mybir as mybir
from concourse.bass_utils import run_bass_kernel_spmd

F32 = mybir.dt.float32
BF16 = mybir.dt.bfloat16
AF = mybir.ActivationFunctionType
ALU = mybir.AluOpType

FULL = dict(S=8192, PAST=2048, NS=16, D=2048, DFF=8192, L=2)
NEG = -30000.0
NO_RECYCLE = False
NORM_EPS = 1e-6
SUBLN_EPS = 1e-5
CHUNK = 64
QT = 512
DH = 128
LAMBDA_INIT1 = 0.8 - 0.6 * math.exp(-0.3 * 1)


def slopes8():
    return [2.0 ** (-8.0 * (i + 1) / 8) for i in range(8)]


def make_consts():
    p = np.arange(128)[:, None]
    f = np.arange(QT)[None, :]
    cb = {}
    cb["ident"] = np.eye(128, dtype=np.float32)
    cb["ones"] = np.ones((128, 128), np.float32)
    i = np.arange(128)[:, None]
    j = np.arange(128)[None, :]
    cb["uincl"] = np.where(i >= j, -1.0, 0.0).astype(np.float32)
    cb["ucomp"] = np.where(i < j, -1.0, 0.0).astype(np.float32)
    for d in range(4):
        cb[f"msb{d}"] = np.where(f > p + 128 * d, 0.0, NEG).astype(np.float32)
        cb[f"mdf{d}"] = np.where(((p + 128 * d) // CHUNK) <= (f // CHUNK), 0.0, NEG).astype(np.float32)
    cb["mtail"] = np.where(p < 16, 0.0, NEG).astype(np.float32) * np.ones((1, QT), np.float32)
    names_b = ["ident", "ones", "uincl", "ucomp"] + [f"msb{d}" for d in range(4)] + \
              [f"mdf{d}" for d in range(4)] + ["mtail"]
    offs_b = {}
    cols = []
    o = 0
    for n in names_b:
        offs_b[n] = o
        o += cb[n].shape[1]
        cols.append(cb[n])
    cbf = np.concatenate(cols, axis=1).astype(ml_dtypes.bfloat16)
    cf = {}
    cf["ident32"] = np.eye(128, dtype=np.float32)
    cf["blin"] = (-(f - p)).astype(np.float32) * np.ones((128, 1), np.float32)
    for d in range(4):
        cf[f"dd{d}"] = (-np.abs(f - p - 128 * d)).astype(np.float32)
    names_f = ["ident32", "blin"] + [f"dd{d}" for d in range(4)]
    offs_f = {}
    cols = []
    o = 0
    for n in names_f:
        offs_f[n] = o
        o += cf[n].shape[1]
        cols.append(cf[n])
    cf32 = np.concatenate(cols, axis=1).astype(np.float32)
    return cbf, offs_b, cf32, offs_f


class SemC:
    _uid = [0]
    _all = []

    def __init__(self, h):
        self.h = h
        self.n = 0
        SemC._uid[0] += 1
        self.uid = SemC._uid[0]
        SemC._all.append(self)


class Ev:
    __slots__ = ("sem", "val")

    def __init__(self, sem, val):
        self.sem = sem
        self.val = val


class EngQ:
    def __init__(self, name):
        self.name = name
        self.ops = []
        self.sem = None
        self.cnt = 0
        self.waited = {}
        self.last = None
        self.trace = []


class Prog:
    ENG = ("pe", "act", "dve", "pool", "sp")

    def __init__(self, nc, es):
        self.nc = nc
        self.es = es
        self.q = {n: EngQ(n) for n in self.ENG}
        self.nsem = 0
        self.free_pool = []
        self.phase_sems = []
        self.proxy = {}
        self.scr = None
        for n in ("pe", "act", "dve", "pool"):
            self.rotate(n)
        self.store_sems = []

    def sem(self, name, local=True):
        if local:
            if self.free_pool and not NO_RECYCLE:
                self.free_pool.sort(key=lambda x: x.n)
                sc = self.free_pool.pop(0)
            else:
                self.nsem += 1
                sc = SemC(self.es.enter_context(self.nc.semaphore(f"{name}_{self.nsem}")))
            self.phase_sems.append(sc)
            return sc
        self.nsem += 1
        return SemC(self.es.enter_context(self.nc.semaphore(f"{name}_{self.nsem}")))

    def end_phase(self):
        self.free_pool.extend(self.phase_sems)
        self.phase_sems = []

    def rotate(self, n):
        q = self.q[n]
        q.sem = self.sem("c_" + n, local=False)
        q.cnt = 0
        q.last = None

    def wait(self, eng, ev):
        if ev is None:
            return
        if isinstance(ev, (list, tuple)):
            for e in ev:
                self.wait(eng, e)
            return
        if eng == "dve" and getattr(ev.sem, "is_dma", False) and self.scr is not None:
            key = (ev.sem.uid, ev.val)
            if key not in self.proxy:
                self.wait("pool", ev)
                scr = self.scr
                self.proxy[key] = self.op("pool", lambda e: e.tensor_copy(out=scr[:, 6:7], in_=scr[:, 7:8]), sig=True)
            ev = self.proxy[key]
        q = self.q[eng]
        k = ev.sem.uid
        if q.waited.get(k, 0) >= ev.val:
            return
        q.waited[k] = ev.val
        h, v = ev.sem.h, ev.val
        q.ops.append(lambda e, h=h, v=v: e.wait_ge(h, v))
        q.trace.append(("w", ev.sem, ev.val, len(q.ops)))

    def op(self, eng, fn, sig=False, waits=()):
        q = self.q[eng]
        for w in waits:
            self.wait(eng, w)
        if sig:
            q.cnt += 1
            sem = q.sem
            q.ops.append(lambda e, fn=fn, h=sem.h: fn(e).then_inc(h, 1))
            q.trace.append(("i", sem, 1, len(q.ops)))
            ev = Ev(sem, q.cnt)
            q.last = ev
            return ev
        q.ops.append(fn)
        return None

    def dma(self, eng, out, in_, semc, waits=()):
        for w in waits:
            self.wait(eng, w)
        semc.n += 16
        semc.is_dma = True
        self.q[eng].ops.append(lambda e, o=out, i=in_, h=semc.h: e.dma_start(out=o, in_=i).then_inc(h, 16))
        self.q[eng].trace.append(("i", semc, 16, len(self.q[eng].ops)))
        return Ev(semc, semc.n)

    def check_deadlock(self):
        vals = {}
        pos = {n: 0 for n in self.ENG}
        progress = True
        while progress:
            progress = False
            for n in self.ENG:
                tr = self.q[n].trace
                while pos[n] < len(tr):
                    kind, sem, v, _ = tr[pos[n]]
                    if kind == "w":
                        if vals.get(sem.uid, 0) >= v:
                            pos[n] += 1
                            progress = True
                        else:
                            break
                    else:
                        vals[sem.uid] = vals.get(sem.uid, 0) + v
                        pos[n] += 1
                        progress = True
        stuck = {n: (pos[n], len(self.q[n].trace)) for n in self.ENG if pos[n] < len(self.q[n].trace)}
        if stuck:
            msg = []
            for n, (p, t) in stuck.items():
                kind, sem, v, opi = self.q[n].trace[p]
                msg.append(f"{n}: stuck at trace {p}/{t} op#{opi} waiting sem(count now {vals.get(sem.uid, 0)}) >= {v}")
            raise RuntimeError("DEADLOCK in sync plan: " + "; ".join(msg))
        return max(vals.values()) if vals else 0


class Arena:
    def __init__(self, ap, ncols):
        self.ap = ap
        self.ncols = ncols
        self.off = 0

    def reset(self):
        self.off = 0

    def take(self, n):
        o = self.off
        self.off += n
        assert self.off <= self.ncols, (self.off, self.ncols)
        return self.ap[:, o:o + n]


class Ring:
    def __init__(self, bufs):
        self.bufs = bufs
        self.free = [None] * len(bufs)
        self.i = 0

    def next(self):
        k = self.i % len(self.bufs)
        self.i += 1
        return k, self.bufs[k], self.free[k]


def build(cfg):
    S, PAST, NS, D, DFF, L = cfg["S"], cfg["PAST"], cfg["NS"], cfg["D"], cfg["DFF"], cfg["L"]
    KC = D // 128
    NH = D // DH
    FC = DFF // 128
    FH = FC // 2
    NQT = S // QT
    SKS = PAST + 128
    NBS = SKS // 128
    cbf_np, OB, cf32_np, OF = make_consts()
    NCB = cbf_np.shape[1]
    NCF = cf32_np.shape[1]
    NHD = NH // 2
    sl = [2.0 ** (-8.0 * (i + 1) / NHD) for i in range(NHD)]
    NCG = 2 * D // 512
    qscale = DH ** -0.5

    nc = bass.Bass("TRN2", target_bir_lowering=False)

    def din(name, shape, dt=F32):
        return nc.dram_tensor(name, list(shape), dt, kind="ExternalInput").ap()

    def dout(name, shape, dt=F32):
        return nc.dram_tensor(name, list(shape), dt, kind="ExternalOutput").ap()

    def dscr(name, shape, dt):
        return nc.dram_tensor(name, list(shape), dt, kind="Internal").ap()

    xT = {"p": din("xT_p", [KC, 128, S]), "s": din("xT_s", [KC, 128, NS])}
    ck = din("ck", [L, PAST, D])
    cv = din("cv", [L, PAST, D])
    wqk_f = [din(f"wqk{l}", [128, 2 * NH * KC * 128]) for l in range(L)]
    wkv_f = [din(f"wkv{l}", [128, (2 * D // 512) * KC * 512]) for l in range(L)]
    wo_f = [din(f"wo{l}", [128, KC * KC * 128]) for l in range(L)]
    wup_f = [din(f"wup{l}", [128, FC * KC * 128]) for l in range(L)]
    wdn_f = [din(f"wdn{l}", [128, 2 * KC * FH * 128]) for l in range(L)]
    gains_d = din("gains", [128, (2 * L + 1) * KC])
    subg_d = din("subg", [128, 2])
    lam_d = din("lam", [128, 4 * 128])
    cbf_d = din("cbf", [128, NCB], BF16)
    cf32_d = din("cf32", [128, NCF])
    y = {"p": dout("y_p", [S, D]), "s": dout("y_s", [NS, D])}
    nk = {"p": dout("nk_p", [L, S, D]), "s": dout("nk_s", [L, NS, D])}
    nv = {"p": dout("nv_p", [L, S, D]), "s": dout("nv_s", [L, NS, D])}
    SQ = {"p": S, "s": NS}
    SK = {"p": S, "s": SKS}
    KOFF = {"p": 0, "s": PAST}
    hT = {k: dscr(f"hT_{k}", [KC, 128, SQ[k]], BF16) for k in "ps"}
    qTs = {k: dscr(f"qT_{k}", [NH, 128, SQ[k]], BF16) for k in "ps"}
    kTs = {k: dscr(f"kT_{k}", [NH, 128, SK[k]], BF16) for k in "ps"}
    vsc = {k: dscr(f"v_{k}", [SK[k], D], BF16) for k in "ps"}
    mixT = {k: dscr(f"mixT_{k}", [KC, 128, SQ[k]], BF16) for k in "ps"}
    xres = {k: dscr(f"xres_{k}", [KC, 128, SQ[k]], F32) for k in "ps"}
    wqk_b = [dscr(f"wqkb{l}", [128, 2 * NH * KC * 128], BF16) for l in range(L)]
    wkv_b = [dscr(f"wkvb{l}", [128, (2 * D // 512) * KC * 512], BF16) for l in range(L)]
    wo_b = [dscr(f"wob{l}", [128, KC * KC * 128], BF16) for l in range(L)]
    wup_b = [dscr(f"wupb{l}", [128, FC * KC * 128], BF16) for l in range(L)]
    wdn_b = [dscr(f"wdnb{l}", [128, 2 * KC * FH * 128], BF16) for l in range(L)]

    es = ExitStack()
    with es:
        P = Prog(nc, es)
        NB_COLS = 57 * 1024
        NF_COLS = 12 * 1024
        arb = Arena(es.enter_context(nc.sbuf_tensor("arena_b", [128, NB_COLS], BF16)), NB_COLS)
        arf = Arena(es.enter_context(nc.sbuf_tensor("arena_f", [128, NF_COLS], F32)), NF_COLS)
        cbf = es.enter_context(nc.sbuf_tensor("cbf_sb", [128, NCB], BF16))
        cf32 = es.enter_context(nc.sbuf_tensor("cf32_sb", [128, NCF], F32))
        gains = es.enter_context(nc.sbuf_tensor("gains_sb", [128, (2 * L + 1) * KC], F32))
        subg = es.enter_context(nc.sbuf_tensor("subg_sb", [128, 2], F32))
        lam_sb = es.enter_context(nc.sbuf_tensor("lam_sb", [128, 4 * 128], F32))
        lam_t = es.enter_context(nc.sbuf_tensor("lam_t", [128, 8], F32))
        scr = es.enter_context(nc.sbuf_tensor("scr", [128, 16], F32))
        zero_b = es.enter_context(nc.sbuf_tensor("zero_b", [128, 2048], BF16))
        ps = [es.enter_context(nc.psum_tensor(f"ps{i}", [128, 512], F32)) for i in range(8)]
        dbg_t = [es.enter_context(nc.sbuf_tensor(f"dbg_t{i}", [128, 512], F32)) for i in range(4)] if cfg.get("debug") else None
        dbg_sem = [None]
        dbg_n = [0]

        def dbg(src, slot, waits):
            if not cfg.get("debug"):
                return
            if dbg_sem[0] is None:
                dbg_sem[0] = P.sem("dbg", local=False)
            t = dbg_t[dbg_n[0] % 4]
            dbg_n[0] += 1
            w = src.shape[1]
            ev = P.op("dve", lambda e: e.tensor_copy(out=t[:, :w], in_=src), sig=True, waits=list(waits))
            r, c = divmod(slot, D // 512)
            evd = P.dma("pool", y["p"][r * 128:(r + 1) * 128, c * 512:c * 512 + w], t[:, :w], dbg_sem[0], waits=[ev])
            for n in Prog.ENG:
                P.wait(n, evd)

        def CB(name, r0=0, r1=128, c0=0, c1=None):
            o = OB[name]
            w = 128 if name in ("ident", "ones", "uincl", "ucomp") else QT
            c1 = w if c1 is None else c1
            return cbf[r0:r1, o + c0:o + c1]

        def CF(name, c0=0, c1=None):
            o = OF[name]
            w = 128 if name == "ident32" else QT
            c1 = w if c1 is None else c1
            return cf32[:, o + c0:o + c1]

        def gcol(idx, c):
            return gains[:, idx * KC + c: idx * KC + c + 1]

        s_const = P.sem("const", local=False)
        ev_c = None
        for (o_, i_) in ((cbf[:, :], cbf_d[:, :]), (cf32[:, :], cf32_d[:, :]), (gains[:, :], gains_d[:, :]),
                         (subg[:, :], subg_d[:, :]), (lam_sb[:, :], lam_d[:, :])):
            ev_c = P.dma("sp", o_, i_, s_const)
        conv_ev = {}

        def conv(name, dst, src, ncols):
            semc = P.sem("cv_" + name, local=False)
            step = 8192
            ev = None
            for c0 in range(0, ncols, step):
                c1 = min(ncols, c0 + step)
                ev = P.dma("pool", dst[:, c0:c1], src[:, c0:c1], semc)
            conv_ev[name] = ev

        def conv_layer(l):
            conv(f"wqk{l}", wqk_b[l], wqk_f[l], 2 * NH * KC * 128)
            conv(f"wkv{l}", wkv_b[l], wkv_f[l], NCG * KC * 512)
            conv(f"wo{l}", wo_b[l], wo_f[l], KC * KC * 128)
            conv(f"wup{l}", wup_b[l], wup_f[l], FC * KC * 128)
            conv(f"wdn{l}", wdn_b[l], wdn_f[l], 2 * KC * FH * 128)

        conv_layer(0)
        ev_z = P.op("pool", lambda e: e.memset(zero_b[:, :], 0.0), sig=True)
        s_zero = P.sem("zero", local=False)
        P.wait("pool", ev_z)
        evz = None
        for h in range(NH):
            evz = P.dma("pool", kTs["s"][h, :, PAST:SKS], zero_b[:, 0:128], s_zero)
        evz = P.dma("pool", vsc["s"][PAST:SKS, :], zero_b[:, 0:D], s_zero)
        setup_evs = [ev_c, evz]
        P.wait("dve", ev_c)
        e1 = P.op("dve", lambda e: e.tensor_tensor(out=lam_sb[:, 0:128], in0=lam_sb[:, 0:128], in1=lam_sb[:, 128:256], op=ALU.mult), sig=True)
        e2 = P.op("dve", lambda e: e.tensor_tensor(out=lam_sb[:, 256:384], in0=lam_sb[:, 256:384], in1=lam_sb[:, 384:512], op=ALU.mult), sig=True)
        e3 = P.op("dve", lambda e: e.reduce_sum(out=lam_t[:, 0:1], in_=lam_sb[:, 0:128], axis=mybir.AxisListType.X), sig=True, waits=[e1])
        e4 = P.op("dve", lambda e: e.reduce_sum(out=lam_t[:, 1:2], in_=lam_sb[:, 256:384], axis=mybir.AxisListType.X), sig=True, waits=[e2])
        e5 = P.op("act", lambda e: e.activation(out=lam_t[:, 2:4], in_=lam_t[:, 0:2], func=AF.Exp), sig=True, waits=[e3, e4])
        e6 = P.op("dve", lambda e: e.tensor_tensor(out=lam_t[:, 4:5], in0=lam_t[:, 3:4], in1=lam_t[:, 2:3], op=ALU.subtract), sig=True, waits=[e5])
        e7 = P.op("dve", lambda e: e.tensor_scalar(out=lam_t[:, 5:6], in0=lam_t[:, 4:5], scalar1=-LAMBDA_INIT1, scalar2=None, op0=ALU.add), sig=True, waits=[e6])
        e8 = P.op("dve", lambda e: e.tensor_scalar(out=subg[:, :], in0=subg[:, :], scalar1=(1.0 - LAMBDA_INIT1), scalar2=None, op0=ALU.mult), sig=True, waits=[e7])
        setup_evs.append(e8)
        NEGLAM = lam_t[:, 5:6]

        store_sems = []

        def barrier(extra=()):
            evs = list(extra)
            for sc in store_sems:
                if sc.n:
                    evs.append(Ev(sc, sc.n))
            m1 = P.op("act", lambda e: e.activation(out=scr[:, 0:1], in_=scr[:, 1:2], func=AF.Copy), sig=True)
            m2 = P.op("dve", lambda e: e.tensor_copy(out=scr[:, 2:3], in_=scr[:, 3:4]), sig=True)
            m3 = P.op("pool", lambda e: e.tensor_copy(out=scr[:, 4:5], in_=scr[:, 5:6]), sig=True)
            evs += [m1, m2, m3]
            if P.q["pe"].last is not None:
                evs.append(P.q["pe"].last)
            for n in Prog.ENG:
                P.wait(n, evs)
            for n in ("pe", "act", "dve", "pool"):
                P.rotate(n)
            P.end_phase()
            del store_sems[:]
            arb.reset()
            arf.reset()

        P.op("pool", lambda e: e.memset(scr[:, :], 0.0))
        barrier(setup_evs)
        P.scr = scr

        def mk_store_sems(n, name):
            r = [P.sem(name) for _ in range(n)]
            store_sems.extend(r)
            return r

        tiles = [("p", i * QT, QT) for i in range(NQT)] + [("s", 0, NS)]

        def rms(xt, N, gidx, out_fn, res, after=(), CS=None):
            bank = res["bank"]
            sqr = res["sq"]
            CS = N if CS is None else CS
            last_mm = None
            for c in range(KC):
                k, sqb, fr = sqr.next()
                ev_sq = P.op("act", lambda e, c=c, sqb=sqb: e.activation(out=sqb[:, :N], in_=xt[:, c * CS:c * CS + N], func=AF.Square),
                             sig=True, waits=list(after) + [fr] + ([res.get("bank_free")] if c == 0 else []))
                last_mm = P.op("pe", lambda e, c=c, sqb=sqb: e.matmul(bank[:, :N], lhsT=CB("ones"), rhs=sqb[:, :N], start=(c == 0), stop=(c == KC - 1)),
                               sig=True, waits=[ev_sq])
                sqr.free[k] = last_mm
            lnv, rstd = res["lnv"], res["rstd"]
            ev_ln = P.op("act", lambda e: e.activation(out=lnv[:, :N], in_=bank[:, :N], func=AF.Ln, scale=1.0 / D, bias=NORM_EPS),
                         sig=True, waits=[last_mm, res.get("rstd_free")])
            ev_rs = P.op("act", lambda e: e.activation(out=rstd[:, :N], in_=lnv[:, :N], func=AF.Exp, scale=-0.5), sig=True, waits=[ev_ln])
            res["bank_free"] = ev_ln
            evs = []
            for c in range(KC):
                dst = out_fn(c)
                evs.append(P.op("dve", lambda e, c=c, dst=dst: e.scalar_tensor_tensor(out=dst, in0=xt[:, c * CS:c * CS + N], scalar=gcol(gidx, c),
                                                                                   in1=rstd[:, :N], op0=ALU.mult, op1=ALU.mult),
                                sig=True, waits=[ev_rs] + list(res.get("out_wait", ()))))
            res["rstd_free"] = evs[-1]
            return evs

        def phase0():
            xt_r = Ring([arf.take(KC * QT)])
            hb_r = Ring([arb.take(KC * QT) for _ in range(2)])
            res = dict(bank=ps[0], sq=Ring([arb.take(QT) for _ in range(2)]), lnv=arf.take(QT), rstd=arf.take(QT))
            s_x = [P.sem("p0x")]
            s_h = mk_store_sems(2, "p0h")
            xt_free = None
            for (sk, t0, N) in tiles:
                _, xt, _ = xt_r.next()
                CS = max(N, 128)
                evl = P.dma("sp", xt[:, :KC * CS].rearrange("p (c n) -> p c n", c=KC)[:, :, :N],
                            xT[sk][:, :, t0:t0 + N].rearrange("c p n -> p c n"), s_x[0], waits=[xt_free])
                k, hb, hfree = hb_r.next()
                res["out_wait"] = [hfree]
                evs = rms(xt, N, 0, lambda c, hb=hb, N=N, CS=CS: hb[:, c * CS:c * CS + N], res, after=[evl], CS=CS)
                xt_free = evs[-1]
                if t0 == 0 and sk == "p":
                    dbg(xt[:, 0:512], 0, evs)
                    dbg(res["lnv"][:, 0:512], 1, evs)
                    dbg(res["rstd"][:, 0:512], 2, evs)
                    dbg(hb[:, 0:512], 3, evs)
                    dbg(res["bank"][:, 0:512], 4, evs)
                evs_st = P.dma("pool", hT[sk][:, :, t0:t0 + N].rearrange("c p n -> p c n"),
                               hb[:, :KC * CS].rearrange("p (c n) -> p c n", c=KC)[:, :, :N], s_h[k], waits=[evs[-1]])
                hb_r.free[k] = evs_st
                if cfg.get("debug") == "rt" and t0 == 0 and sk == "p":
                    rt = arb.take(1024)
                    srt = P.sem("rt")
                    e1 = P.dma("sp", rt[:, 0:512], hT["p"][0, :, 0:512], srt, waits=[evs_st])
                    e2 = P.dma("sp", rt[:, 512:1024], wkv_b[0][:, 0:512], srt, waits=[conv_ev["wkv0"]])
                    dbg(rt[:, 0:512], 0, [e1, e2])
                    dbg(rt[:, 512:1024], 1, [e1, e2])

        STOP = cfg.get("stop", "")
        if STOP != "setup":
            phase0()
            barrier()

        def phase1(l):
            hb_r = Ring([arb.take(KC * QT) for _ in range(2)])
            wA_r = Ring([arb.take(KC * 128) for _ in range(3)])
            wB_r = Ring([arb.take(KC * 512) for _ in range(2)])
            stA_r = Ring([arb.take(QT) for _ in range(3)])
            stVb_r = Ring([arb.take(512) for _ in range(2)])
            stB_r = Ring([arf.take(512) for _ in range(3)])
            ckb = arf.take(4 * D) if True else None
            stT_r = Ring([arb.take(512) for _ in range(2)])
            s_stV = mk_store_sems(2, "p1sV")
            s_h = [P.sem("p1h") for _ in range(2)]
            s_wA = [P.sem("p1wA") for _ in range(3)]
            s_wB = [P.sem("p1wB") for _ in range(2)]
            s_stA = mk_store_sems(3, "p1sA")
            s_stB = mk_store_sems(3, "p1sB")
            s_stT = mk_store_sems(2, "p1sT")
            s_ck = P.sem("p1ck")
            s_cvc = P.sem("p1cv")
            store_sems.append(s_cvc)
            psA = Ring([ps[0], ps[1]])
            psB = Ring([ps[2], ps[3], ps[4]])
            psT = Ring([ps[5], ps[6]])
            SKIP = set(cfg.get("skip", "").split(","))
            if cfg.get("debug") == "p1":
                dbg(cf32[:, OF["blin"]:OF["blin"] + 512], 5, [])
            if "cv" not in SKIP:
                for r0 in range(0, PAST, 128):
                    P.dma("pool", vsc["s"][r0:r0 + 128, :], cv[l, r0:r0 + 128, :], s_cvc)
            ck_free = None
            for g0 in range(0, PAST // 128 if "ck" not in SKIP else 0, 4):
                nb = min(4, PAST // 128 - g0)
                evl = None
                for b in range(nb):
                    evl = P.dma("sp", ckb[:, b * D:(b + 1) * D], ck[l, (g0 + b) * 128:(g0 + b + 1) * 128, :], s_ck, waits=[ck_free])
                last_tr = None
                for h in range(NH):
                    kb_, bank, bfree = psT.next()
                    tr = None
                    for b in range(nb):
                        tr = P.op("pe", lambda e, b=b, h=h, bank=bank: e.transpose(bank[:, b * 128:(b + 1) * 128], ckb[:, b * D + h * 128: b * D + (h + 1) * 128], CF("ident32")),
                                  sig=(b == nb - 1), waits=[evl, bfree])
                    last_tr = tr
                    ks, st, sfree = stT_r.next()
                    evc = P.op("dve", lambda e, st=st, bank=bank, nb=nb: e.tensor_copy(out=st[:, :nb * 128], in_=bank[:, :nb * 128]), sig=True, waits=[tr, sfree])
                    psT.free[kb_] = evc
                    stT_r.free[ks] = P.dma("pool", kTs["s"][h, :, g0 * 128:(g0 + nb) * 128], st[:, :nb * 128], s_stT[ks], waits=[evc])
                ck_free = last_tr
            for (sk, t0, N) in tiles:
                kh, hb, hfree = hb_r.next()
                CS = max(N, 128)
                ev_h = P.dma("sp", hb[:, :KC * CS].rearrange("p (c n) -> p c n", c=KC)[:, :, :N], hT[sk][:, :, t0:t0 + N].rearrange("c p n -> p c n"),
                             s_h[kh], waits=[hfree])
                last_pe = None
                if cfg.get("debug") == "p1" and sk == "p" and t0 == 0:
                    dbg(hb[:, 0:512], 6, [ev_h])
                for oc in range(2 * NH if "A" not in SKIP else 0):
                    kw, wA, wfree = wA_r.next()
                    ev_w = P.dma("sp", wA, wqk_b[l][:, oc * KC * 128:(oc + 1) * KC * 128], s_wA[kw], waits=[wfree, conv_ev[f"wqk{l}"]])
                    kb_, bank, bfree = psA.next()
                    mm = None
                    for kc in range(KC):
                        mm = P.op("pe", lambda e, kc=kc, wA=wA, bank=bank, hb=hb, N=N, CS=CS: e.matmul(bank[:, :N], lhsT=wA[:, kc * 128:(kc + 1) * 128], rhs=hb[:, kc * CS:kc * CS + N],
                                                                                                 start=(kc == 0), stop=(kc == KC - 1)),
                                  sig=(kc == KC - 1), waits=[ev_w, ev_h, bfree])
                    wA_r.free[kw] = mm
                    last_pe = mm
                    ks, st, sfree = stA_r.next()
                    sc = qscale if oc < NH else 1.0
                    if oc % 2 == 0:
                        evc = P.op("dve", lambda e, st=st, bank=bank, N=N, sc=sc: e.tensor_scalar(out=st[:, :N], in0=bank[:, :N], scalar1=sc, scalar2=None, op0=ALU.mult),
                                   sig=True, waits=[mm, sfree])
                    else:
                        evc = P.op("act", lambda e, st=st, bank=bank, N=N, sc=sc: e.activation(out=st[:, :N], in_=bank[:, :N], func=AF.Copy, scale=sc),
                                   sig=True, waits=[mm, sfree])
                    psA.free[kb_] = evc
                    if oc < NH:
                        dst = qTs[sk][oc, :, t0:t0 + N]
                    else:
                        dst = kTs[sk][oc - NH, :, KOFF[sk] + t0:KOFF[sk] + t0 + N]
                    stA_r.free[ks] = P.dma("pool", dst, st[:, :N], s_stA[ks], waits=[evc])
                for cg in range((NCG // 2 if "V" in SKIP else NCG) if "B" not in SKIP else 0):
                    if "Bs" in SKIP and sk == "s":
                        continue
                    if "Bp" in SKIP and sk == "p":
                        continue
                    kw, wB, wfree = wB_r.next()
                    ev_w = P.dma("sp", wB, wkv_b[l][:, cg * KC * 512:(cg + 1) * KC * 512], s_wB[kw], waits=[wfree, conv_ev[f"wkv{l}"]])
                    mm = None
                    if cfg.get("debug") == "p1" and sk == "p" and t0 == 0 and cg == 0:
                        dbg(wB[:, 0:512], 7, [ev_w])
                    for tb in range((N + 127) // 128):
                        nt = min(128, N - tb * 128)
                        kb_, bank, bfree = psB.next()
                        for kc in range(KC):
                            mm = P.op("pe", lambda e, kc=kc, wB=wB, bank=bank, hb=hb, N=N, tb=tb, nt=nt, CS=CS: e.matmul(
                                bank[:, :], lhsT=hb[:, kc * CS + tb * 128: kc * CS + tb * 128 + 128], rhs=wB[:, kc * 512:(kc + 1) * 512],
                                start=(kc == 0), stop=(kc == KC - 1)), sig=(kc == KC - 1), waits=[ev_w, ev_h, bfree])
                        last_pe = mm
                        if cfg.get("debug") == "p1" and sk == "p" and t0 == 0 and cg == 0 and tb == 0:
                            dbg(bank[:, 0:512], 4, [mm])
                        ks, st, sfree = stB_r.next()
                        evc = P.op("act", lambda e, st=st, bank=bank, nt=nt: e.activation(out=st[:nt, :], in_=bank[:nt, :], func=AF.Copy), sig=True, waits=[mm, sfree])
                        if cfg.get("debug") == "p1" and sk == "p" and t0 == 0 and cg == 0 and tb == 0:
                            dbg(st[:, 0:512], 0, [evc])
                            dbg(hb[:, 0:512], 1, [evc])
                            dbg(wB[:, 0:512], 2, [evc])
                        outt = nk[sk] if cg < NCG // 2 else nv[sk]
                        c0 = (cg % (NCG // 2)) * 512
                        stB_r.free[ks] = P.dma("pool", outt[l, t0 + tb * 128:t0 + tb * 128 + nt, c0:c0 + 512], st[:nt, :], s_stB[ks], waits=[evc])
                        if cg >= NCG // 2 and "Vcp" not in SKIP:
                            kv_, stv, vfree = stVb_r.next()
                            evv = P.op("dve", lambda e, stv=stv, st=st, nt=nt: e.tensor_copy(out=stv[:nt, :], in_=st[:nt, :]), sig=True, waits=[evc, vfree])
                            stB_r.free[ks] = [stB_r.free[ks], evv]
                            r0 = KOFF[sk] + t0 + tb * 128
                            if "Vst" not in SKIP:
                                _evs = P.dma(cfg.get("vq", "pool"), vsc[sk][r0:r0 + nt, c0:c0 + 512], stv[:nt, :], s_stV[kv_], waits=[evv])
                                if "Vnofree" not in SKIP:
                                    stVb_r.free[kv_] = _evs
                            psB.free[kb_] = evc
                        else:
                            psB.free[kb_] = evc
                    wB_r.free[kw] = mm
                hb_r.free[kh] = last_pe

        def attn_loads(l, per):
            NU = NH // per
            nbuf = 2 if per == 1 else 1
            KT_r = Ring([arb.take(per * S) for _ in range(nbuf)])
            V_r = Ring([arb.take(per * S) for _ in range(nbuf)])
            KTs_r = Ring([arb.take(per * SKS) for _ in range(nbuf)])
            Vs_r = Ring([arb.take(per * SKS) for _ in range(nbuf)])
            s_k = [P.sem("p2k") for _ in range(nbuf)]
            s_v = [P.sem("p2v") for _ in range(nbuf)]
            s_ks = [P.sem("p2ks") for _ in range(nbuf)]
            s_vs = [P.sem("p2vs") for _ in range(nbuf)]
            return NU, KT_r, V_r, KTs_r, Vs_r, s_k, s_v, s_ks, s_vs

        def load_unit(u, per, rings):
            NU, KT_r, V_r, KTs_r, Vs_r, s_k, s_v, s_ks, s_vs = rings
            k1, KT, f1 = KT_r.next()
            k2, V, f2 = V_r.next()
            k3, KTs_, f3 = KTs_r.next()
            k4, Vs_, f4 = Vs_r.next()
            ev = []
            for m in range(per):
                ev.append(P.dma("sp", KT[:, m * S:(m + 1) * S], kTs["p"][u * per + m, :, :], s_k[k1], waits=[f1]))
                ev.append(P.dma("sp", KTs_[:, m * SKS:(m + 1) * SKS], kTs["s"][u * per + m, :, :], s_ks[k3], waits=[f3]))
            w = per * 128
            for b0 in range(0, S // 128, 8):
                b1 = min(S // 128, b0 + 8)
                ev.append(P.dma("sp", V[:, b0 * w:b1 * w].rearrange("p (b d) -> p b d", d=w),
                                vsc["p"].rearrange("(b p) d -> p b d", p=128)[:, b0:b1, u * w:(u + 1) * w], s_v[k2], waits=[f2]))
            for b0 in range(0, SKS // 128, 8):
                b1 = min(SKS // 128, b0 + 8)
                ev.append(P.dma("sp", Vs_[:, b0 * w:b1 * w].rearrange("p (b d) -> p b d", d=w),
                                vsc["s"].rearrange("(b p) d -> p b d", p=128)[:, b0:b1, u * w:(u + 1) * w], s_vs[k4], waits=[f4]))
            return (k1, k2, k3, k4), KT, V, KTs_, Vs_, ev

        def groups_of_unit():
            gs = []
            for i in range(NQT):
                blks = []
                for kb in range(4 * i + 3, -1, -1):
                    d = kb - 4 * i
                    blks.append((kb, d if d >= 0 else None, d))
                gs.append(("p", i * QT, QT, blks))
            blks = [(PAST // 128, "tail", 0)] + [(kb, None, kb - PAST // 128) for kb in range(PAST // 128 - 1, -1, -1)]
            gs.append(("s", 0, NS, blks))
            return gs

        def phase2_sb(l):
            rings = attn_loads(l, 1)
            NU = rings[0]
            Q_r = Ring([arb.take(QT) for _ in range(3)])
            s_q = [P.sem("p2q") for _ in range(3)]
            NE, NSP, NEG_, NW = 4, 4, 2, 3
            e_b = [arf.take(QT) for _ in range(NE)]
            eg_b = [arf.take(QT) for _ in range(NEG_)]
            sp_b = [arb.take(QT) for _ in range(NSP)]
            w_b = [arb.take(QT) for _ in range(NW)]
            mst_r = Ring([arb.take(QT) for _ in range(2)])
            s_m = mk_store_sems(2, "p2m")
            zb = [ps[0], ps[1], ps[2]]
            G = ps[3]
            ob = [ps[4], ps[5]]
            items = []
            groups = []
            unit_loads = {}
            gl = groups_of_unit()
            for u in range(NU):
                for gi, (sk, t0, N, blks) in enumerate(gl):
                    g = dict(u=u, sk=sk, t0=t0, N=N, first=len(items), idx=len(groups), gi=gi)
                    for bi, (kb, d, drel) in enumerate(blks):
                        items.append(dict(g=g, kb=kb, d=d, first=(bi == 0), last=(bi == len(blks) - 1)))
                    g["last"] = len(items) - 1
                    groups.append(g)
            if cfg.get("p2g") is not None:
                groups = groups[:int(cfg["p2g"])]
                items = items[:(groups[-1]["last"] + 1) if groups else 0]
            n_it = len(items)
            evZ = [None] * n_it
            evE = [None] * n_it
            evSP = [None] * n_it
            evGU = [None] * n_it
            evEG = [None] * n_it
            evGC = [None] * n_it
            evW = [None] * n_it
            evPV = [None] * n_it
            evEV = {}
            unit_bufs = {}
            unit_keys = {}
            q_of_group = {}

            def ensure_unit(u):
                if u in unit_bufs or u >= NU:
                    return
                keys, KT, V, KTs_, Vs_, ev = load_unit(u, 1, rings)
                unit_bufs[u] = (KT, V, KTs_, Vs_, ev)
                unit_keys[u] = keys

            def ensure_q(gidx):
                if gidx in q_of_group or gidx >= len(groups):
                    return
                g = groups[gidx]
                k, qb, fr = Q_r.next()
                ev = P.dma("sp", qb[:, :g["N"]], qTs[g["sk"]][g["u"], :, g["t0"]:g["t0"] + g["N"]], s_q[k], waits=[fr])
                q_of_group[gidx] = (k, qb, ev)

            ensure_unit(0)

            def srcs(it):
                g = it["g"]
                KT, V, KTs_, Vs_, ev = unit_bufs[g["u"]]
                kb = it["kb"]
                if g["sk"] == "p":
                    return KT[:, kb * 128:(kb + 1) * 128], V[:, kb * 128:(kb + 1) * 128], ev
                return KTs_[:, kb * 128:(kb + 1) * 128], Vs_[:, kb * 128:(kb + 1) * 128], ev

            def do_Z(n):
                it = items[n]
                g = it["g"]
                N = g["N"]
                ensure_unit(g["u"])
                if it["first"]:
                    ensure_q(g["idx"])
                    ensure_q(g["idx"] + 1)
                    if g["gi"] == 1:
                        ensure_unit(g["u"] + 1)
                kq, qb, evq = q_of_group[g["idx"]]
                kt, v, evl = srcs(it)
                z = zb[n % 3]
                masked = it["d"] is not None
                wz = [evq] + evl + [evE[n - 3] if n >= 3 else None]
                mname = None
                if masked:
                    mname = "msb0" if it["d"] == "tail" else f"msb{it['d']}"
                ev = P.op("pe", lambda e: e.matmul(z[:, :N], lhsT=kt, rhs=qb[:, :N], start=True, stop=not masked), sig=not masked, waits=wz)
                if masked:
                    ev = P.op("pe", lambda e: e.matmul(z[:, :N], lhsT=CB("ident"), rhs=CB(mname, c1=N), start=False, stop=True), sig=True)
                evZ[n] = ev
                if it["last"]:
                    Q_r.free[kq] = ev
                if cfg.get("debug") == "p2" and n == 0:
                    dbg(z[:, 0:512], 0, [ev])

            def do_E(n):
                N = items[n]["g"]["N"]
                z = zb[n % 3]
                eb = e_b[n % NE]
                evE[n] = P.op("act", lambda e: e.activation(out=eb[:, :N], in_=z[:, :N], func=AF.Exp), sig=True,
                              waits=[evZ[n], evW[n - NE] if n >= NE else None])
                if cfg.get("debug") == "p2" and n == 0:
                    dbg(eb[:, 0:512], 1, [evE[n]])

            def do_SP(n):
                N = items[n]["g"]["N"]
                eb = e_b[n % NE]
                spb = sp_b[n % NSP]
                prev = n - NSP
                wfree = None
                if prev >= 0:
                    wfree = evGU[prev] if items[prev]["last"] else evGU[prev + 1]
                    assert wfree is not None or "U" not in STG
                evSP[n] = P.op("act", lambda e: e.activation(out=spb[:, :N], in_=eb[:, :N], func=AF.Ln, bias=1.0), sig=True, waits=[evE[n], wfree])
                if cfg.get("debug") == "p2" and n == 0:
                    dbg(spb[:, 0:512], 2, [evSP[n]])

            def do_GU(n):
                it = items[n]
                N = it["g"]["N"]
                spb = sp_b[n % NSP]
                wz = [evSP[n]]
                assert evSP[n] is not None or "S" not in STG
                if n >= 1 and not cfg.get("nogw"):
                    assert evEG[n - 1] is not None or "G" not in STG
                    wz.append(evEG[n - 1])
                evGU[n] = P.op("pe", lambda e: e.matmul(G[:, :N], lhsT=CB("uincl"), rhs=spb[:, :N], start=it["first"], stop=True), sig=True, waits=wz)
                if cfg.get("debug") == "p2" and n == 0:
                    dbg(G[:, 0:512], 3, [evGU[n]])

            def do_EG(n):
                N = items[n]["g"]["N"]
                egb = eg_b[n % NEG_]
                _src = zb[n % 3] if cfg.get("egsrc") == "z" else G
                _fn = AF.Copy if cfg.get("egfn") == "copy" else AF.Exp
                evEG[n] = P.op("act", lambda e: e.activation(out=egb[:, :N], in_=_src[:, :N], func=_fn), sig=True,
                               waits=[evGU[n], evW[n - NEG_] if n >= NEG_ else None])
                if cfg.get("debug") == "p2" and n == 0:
                    dbg(egb[:, 0:512], 4, [evEG[n]])

            def do_GC(n):
                it = items[n]
                if it["last"]:
                    return
                N = it["g"]["N"]
                spb = sp_b[n % NSP]
                assert evEG[n] is not None or "G" not in STG
                P.op("pe", lambda e: e.matmul(G[:, :N], lhsT=CB("ucomp"), rhs=spb[:, :N], start=False, stop=True), sig=False, waits=[evEG[n]])

            def do_W(n):
                N = items[n]["g"]["N"]
                eb = e_b[n % NE]
                egb = eg_b[n % NEG_]
                wb = w_b[n % NW]
                evW[n] = P.op("dve", lambda e: e.tensor_tensor(out=wb[:, :N], in0=eb[:, :N], in1=egb[:, :N], op=ALU.mult), sig=True,
                              waits=[evEG[n], evE[n], evPV[n - NW] if n >= NW else None])
                if cfg.get("debug") == "p2" and n == 0:
                    dbg(wb[:, 0:512], 5, [evW[n]])

            def do_PV(n):
                it = items[n]
                g = it["g"]
                N = g["N"]
                kt, v, evl = srcs(it)
                wb = w_b[n % NW]
                o = ob[g["idx"] % 2]
                wz = [evW[n]] + evl
                if it["first"] and g["idx"] >= 2:
                    wz.append(evEV[g["idx"] - 2])
                evPV[n] = P.op("pe", lambda e: e.matmul(o[:, :N], lhsT=v, rhs=wb[:, :N], start=it["first"], stop=it["last"]), sig=True, waits=wz)
                if it["last"]:
                    if g["idx"] + 1 >= len(groups) or groups[g["idx"] + 1]["u"] != g["u"]:
                        keys = unit_keys[g["u"]]
                        for ring, k in zip((rings[1], rings[2], rings[3], rings[4]), keys):
                            ring.free[k] = evPV[n]

            def do_EV(gidx):
                g = groups[gidx]
                N = g["N"]
                o = ob[gidx % 2]
                k, mst, fr = mst_r.next()
                evEV[gidx] = P.op("dve", lambda e: e.tensor_copy(out=mst[:, :N], in_=o[:, :N]), sig=True, waits=[evPV[g["last"]], fr])
                mst_r.free[k] = P.dma("pool", mixT[g["sk"]][g["u"], :, g["t0"]:g["t0"] + N], mst[:, :N], s_m[k], waits=[evEV[gidx]])

            ends = {g["last"]: g["idx"] for g in groups}
            STG = cfg.get("p2st", "ZESUGCWPV")
            for s in range(-3, n_it + 3):
                if 0 <= s + 3 < n_it and "Z" in STG:
                    do_Z(s + 3)
                if 0 <= s + 2 < n_it and "E" in STG:
                    do_E(s + 2)
                if 0 <= s < n_it and "G" in STG:
                    do_EG(s)
                if 0 <= s + 2 < n_it and "S" in STG:
                    do_SP(s + 2)
                if 0 <= s < n_it and "C" in STG:
                    do_GC(s)
                if 0 <= s + 1 < n_it and "U" in STG:
                    do_GU(s + 1)
                if 0 <= s < n_it and "W" in STG:
                    do_W(s)
                if 0 <= s - 1 < n_it and "P" in STG:
                    do_PV(s - 1)
                if (s - 2) in ends and "V" in STG:
                    do_EV(ends[s - 2])

        def phase2_diff(l):
            rings = attn_loads(l, 2)
            NU = rings[0]
            Q_r = Ring([arb.take(2 * QT) for _ in range(3)])
            s_q = [P.sem("p2q") for _ in range(3)]
            NT, NP = 3, 4
            t_b = [arf.take(QT) for _ in range(NT)]
            p_b = [arb.take(QT) for _ in range(NP)]
            acc = [arf.take(QT) for _ in range(6)]
            tmp = [arf.take(QT) for _ in range(4)]
            osb = [arf.take(QT) for _ in range(2)]
            sqb = [arb.take(QT) for _ in range(2)]
            mst_r = Ring([arb.take(2 * QT) for _ in range(2)])
            s_m = mk_store_sems(2, "p2m")
            zb = [ps[0], ps[1]]
            den = [ps[2], ps[3]]
            ob = [[ps[4], ps[5]], [ps[6], ps[7]]]
            items = []
            groups = []
            gl = groups_of_unit()
            for u in range(NU):
                for gi, (sk, t0, N, blks) in enumerate(gl):
                    g = dict(u=u, sk=sk, t0=t0, N=N, first=len(items), idx=len(groups))
                    for bi, (kb, d, drel) in enumerate(blks):
                        for m in range(2):
                            items.append(dict(g=g, kb=kb, d=d, drel=drel, m=m, first=(bi == 0), last=(bi == len(blks) - 1)))
                    g["last"] = len(items) - 1
                    groups.append(g)
            n_it = len(items)
            evZ = [None] * n_it
            evT = [None] * n_it
            evE = [None] * n_it
            evPV = [None] * n_it
            evACC = {}
            evFIN = {}
            unit_bufs = {}
            unit_keys = {}
            q_of_group = {}

            def ensure_unit(u):
                if u in unit_bufs or u >= NU:
                    return
                keys, KT, V, KTs_, Vs_, ev = load_unit(u, 2, rings)
                unit_bufs[u] = (KT, V, KTs_, Vs_, ev)
                unit_keys[u] = keys

            def ensure_q(gidx):
                if gidx in q_of_group or gidx >= len(groups):
                    return
                g = groups[gidx]
                k, qb, fr = Q_r.next()
                N = g["N"]
                ev = None
                for m in range(2):
                    ev = P.dma("sp", qb[:, m * QT:m * QT + N], qTs[g["sk"]][g["u"] * 2 + m, :, g["t0"]:g["t0"] + N], s_q[k], waits=[fr])
                q_of_group[gidx] = (k, qb, ev)

            ensure_unit(0)

            def srcs(it):
                g = it["g"]
                KT, V, KTs_, Vs_, ev = unit_bufs[g["u"]]
                kb, m = it["kb"], it["m"]
                if g["sk"] == "p":
                    return KT[:, m * S + kb * 128: m * S + (kb + 1) * 128], V[:, kb * 256:(kb + 1) * 256], ev
                return KTs_[:, m * SKS + kb * 128: m * SKS + (kb + 1) * 128], Vs_[:, kb * 256:(kb + 1) * 256], ev

            def do_Z(n):
                it = items[n]
                g = it["g"]
                N = g["N"]
                ensure_unit(g["u"])
                if it["first"] and it["m"] == 0:
                    ensure_q(g["idx"])
                    ensure_q(g["idx"] + 1)
                kq, qb, evq = q_of_group[g["idx"]]
                kt, v, evl = srcs(it)
                z = zb[n % 2]
                m = it["m"]
                masked = it["d"] is not None
                mname = None
                if masked:
                    mname = "mtail" if it["d"] == "tail" else f"mdf{it['d']}"
                wz = [evq] + evl + [evT[n - 2] if n >= 2 else None]
                ev = P.op("pe", lambda e: e.matmul(z[:, :N], lhsT=kt, rhs=qb[:, m * QT:m * QT + N], start=True, stop=not masked), sig=not masked, waits=wz)
                if masked:
                    ev = P.op("pe", lambda e: e.matmul(z[:, :N], lhsT=CB("ident"), rhs=CB(mname, c1=N), start=False, stop=True), sig=True)
                evZ[n] = ev
                if it["last"] and m == 1:
                    Q_r.free[kq] = ev

            def do_T(n):
                it = items[n]
                N = it["g"]["N"]
                z = zb[n % 2]
                tb_ = t_b[n % NT]
                slope = sl[it["g"]["u"]]
                if it["d"] is None:
                    dt_ = CF("blin", c1=N)
                elif it["d"] == "tail":
                    dt_ = CF("dd0", c1=N)
                else:
                    dt_ = CF(f"dd{it['d']}", c1=N)
                evT[n] = P.op("dve", lambda e: e.scalar_tensor_tensor(out=tb_[:, :N], in0=dt_, scalar=slope, in1=z[:, :N], op0=ALU.mult, op1=ALU.add),
                              sig=True, waits=[evZ[n], evE[n - NT] if n >= NT else None])

            def do_E(n):
                it = items[n]
                N = it["g"]["N"]
                tb_ = t_b[n % NT]
                pb = p_b[n % NP]
                slope = sl[it["g"]["u"]]
                bias = 0.0
                if it["d"] is None:
                    bias = slope * 128.0 * it["drel"]
                evE[n] = P.op("act", lambda e: e.activation(out=pb[:, :N], in_=tb_[:, :N], func=AF.Exp, bias=bias), sig=True,
                              waits=[evT[n], evPV[n - NP] if n >= NP else None])

            def do_PV(n):
                it = items[n]
                g = it["g"]
                N = g["N"]
                m = it["m"]
                kt, v, evl = srcs(it)
                pb = p_b[n % NP]
                wz = [evE[n]] + evl
                if it["first"] and g["idx"] >= 1:
                    wz.append(evACC[g["idx"] - 1])
                P.op("pe", lambda e: e.matmul(den[m][:, :N], lhsT=CB("ones"), rhs=pb[:, :N], start=it["first"], stop=it["last"]), waits=wz)
                P.op("pe", lambda e: e.matmul(ob[m][0][:, :N], lhsT=v[:, 0:128], rhs=pb[:, :N], start=it["first"], stop=it["last"]))
                evPV[n] = P.op("pe", lambda e: e.matmul(ob[m][1][:, :N], lhsT=v[:, 128:256], rhs=pb[:, :N], start=it["first"], stop=it["last"]), sig=True)
                if it["last"] and m == 1:
                    if g["idx"] + 1 >= len(groups) or groups[g["idx"] + 1]["u"] != g["u"]:
                        keys = unit_keys[g["u"]]
                        for ring, k in zip((rings[1], rings[2], rings[3], rings[4]), keys):
                            ring.free[k] = evPV[n]

            def do_FIN(gidx):
                g = groups[gidx]
                N = g["N"]
                prevfin = evFIN.get(gidx - 1)
                lastpv = evPV[g["last"]]
                srcs_ = [den[0], den[1], ob[0][0], ob[0][1], ob[1][0], ob[1][1]]
                evs = []
                for i, sps_ in enumerate(srcs_):
                    if i in (0, 1):
                        evs.append(P.op("dve", lambda e, i=i, sps_=sps_: e.reciprocal(out=acc[i][:, :N], in_=sps_[:, :N]), sig=True, waits=[lastpv, prevfin]))
                    elif i % 2 == 0:
                        evs.append(P.op("act", lambda e, i=i, sps_=sps_: e.activation(out=acc[i][:, :N], in_=sps_[:, :N], func=AF.Copy), sig=True, waits=[lastpv, prevfin]))
                    else:
                        evs.append(P.op("dve", lambda e, i=i, sps_=sps_: e.tensor_copy(out=acc[i][:, :N], in_=sps_[:, :N]), sig=True, waits=[lastpv, prevfin]))
                evACC[gidx] = evs
                e_r1 = P.op("dve", lambda e: e.tensor_scalar(out=acc[1][:, :N], in0=acc[1][:, :N], scalar1=NEGLAM, scalar2=None, op0=ALU.mult), sig=True, waits=evs)
                eo = []
                for c in range(2):
                    ea = P.op("pool", lambda e, c=c: e.tensor_tensor(out=tmp[c][:, :N], in0=acc[2 + c][:, :N], in1=acc[0][:, :N], op=ALU.mult), sig=True, waits=evs + [e_r1])
                    eb_ = P.op("dve", lambda e, c=c: e.tensor_tensor(out=tmp[2 + c][:, :N], in0=acc[4 + c][:, :N], in1=acc[1][:, :N], op=ALU.mult), sig=True, waits=[e_r1])
                    eo.append(P.op("dve", lambda e, c=c: e.tensor_tensor(out=osb[c][:, :N], in0=tmp[c][:, :N], in1=tmp[2 + c][:, :N], op=ALU.add), sig=True, waits=[ea, eb_]))
                mm = None
                for c in range(2):
                    esq = P.op("pool", lambda e, c=c: e.tensor_tensor(out=sqb[c][:, :N], in0=osb[c][:, :N], in1=osb[c][:, :N], op=ALU.mult), sig=True, waits=[eo[c]])
                    mm = P.op("pe", lambda e, c=c: e.matmul(ssb[:, :N], lhsT=CB("ones"), rhs=sqb[c][:, :N], start=(c == 0), stop=(c == 1)), sig=True, waits=[esq, ss_free[0]])
                eln = P.op("act", lambda e: e.activation(out=tmp[0][:, :N], in_=ssb[:, :N], func=AF.Ln, scale=1.0 / 256.0, bias=SUBLN_EPS), sig=True, waits=[mm] + eo)
                ss_free[0] = eln
                ers = P.op("act", lambda e: e.activation(out=tmp[1][:, :N], in_=tmp[0][:, :N], func=AF.Exp, scale=-0.5), sig=True, waits=[eln])
                k, mst, fr = mst_r.next()
                ef = None
                for c in range(2):
                    ef = P.op("dve", lambda e, c=c: e.scalar_tensor_tensor(out=mst[:, c * QT:c * QT + N], in0=osb[c][:, :N], scalar=subg[:, c:c + 1], in1=tmp[1][:, :N],
                                                                       op0=ALU.mult, op1=ALU.mult), sig=True, waits=[ers, fr])
                evFIN[gidx] = ef
                if cfg.get("debug") == "p2d" and gidx == int(cfg.get("dbg_g", 0)):
                    dbg(osb[0][:, 0:N], 0, [ef])
                    dbg(mst[:, 0:N], 1, [ef])
                    dbg(tmp[1][:, 0:N], 2, [ef])
                    dbg(acc[0][:, 0:N], 3, [ef])
                    dbg(acc[1][:, 0:N], 4, [ef])
                evl = None
                for c in range(2):
                    evl = P.dma("pool", mixT[g["sk"]][g["u"] * 2 + c, :, g["t0"]:g["t0"] + N], mst[:, c * QT:c * QT + N], s_m[k], waits=[ef])
                mst_r.free[k] = evl

            ssb = ob[1][1]
            ss_free = [None]
            ends = {g["last"]: g["idx"] for g in groups}
            orig_do_PV = do_PV

            def do_PV2(n):
                it = items[n]
                if it["first"] and it["g"]["idx"] >= 1 and it["m"] == 0:
                    P.wait("pe", ss_free[0])
                orig_do_PV(n)

            unit_ranges = {}
            for n_, it_ in enumerate(items):
                u_ = it_["g"]["u"]
                if u_ not in unit_ranges:
                    unit_ranges[u_] = [n_, n_ + 1]
                unit_ranges[u_][1] = n_ + 1
            for u_ in sorted(unit_ranges):
                a_, b_ = unit_ranges[u_]
                if u_ > 0:
                    _last = P.q["pe"].last
                    P.rotate("pe")
                    P.q["pe"].last = _last
                for s in range(a_ - 2, b_ + 3):
                    if a_ <= s + 2 < b_:
                        do_Z(s + 2)
                    if a_ <= s + 1 < b_:
                        do_T(s + 1)
                    if a_ <= s < b_:
                        do_E(s)
                    if a_ <= s - 1 < b_:
                        do_PV2(s - 1)
                        if (s - 1) in ends:
                            do_FIN(ends[s - 1])

        def phase3(l):
            last = (l == L - 1)
            mx_r = Ring([arb.take(KC * QT) for _ in range(2)])
            h2 = arb.take(KC * QT)
            a_t = arb.take(FH * QT)
            wA_r = Ring([arb.take(KC * 128) for _ in range(3)])
            wD_r = Ring([arb.take(FH * 128) for _ in range(2)])
            sq_r = Ring([arb.take(QT) for _ in range(2)])
            xt = arf.take(KC * QT)
            rst_r = Ring([arf.take(QT) for _ in range(2)])
            lnv = arf.take(QT)
            rstd = arf.take(QT)
            yst_r = Ring([arf.take(512) for _ in range(2)])
            s_mx = [P.sem("p3mx") for _ in range(2)]
            s_x = P.sem("p3x")
            s_wA = [P.sem("p3wA") for _ in range(3)]
            s_wD = [P.sem("p3wD") for _ in range(2)]
            s_xr = mk_store_sems(1, "p3xr")[0]
            s_hh = mk_store_sems(1, "p3hh")[0]
            s_y = mk_store_sems(2, "p3y")
            psR = Ring([ps[0], ps[1], ps[2], ps[3]])
            res = dict(bank=ps[4], sq=sq_r, lnv=lnv, rstd=rstd)
            psY = Ring([ps[5], ps[6]])
            xt_free = None
            h2_free = None
            a_free = [None]
            for (sk, t0, N) in tiles:
                km, mx, mfree = mx_r.next()
                CS = max(N, 128)
                ev_mx = P.dma("sp", mx[:, :KC * N].rearrange("p (c n) -> p c n", c=KC), mixT[sk][:, :, t0:t0 + N].rearrange("c p n -> p c n"),
                              s_mx[km], waits=[mfree])
                xsrc = xT[sk] if l == 0 else xres[sk]
                ev_x = P.dma("sp", xt[:, :KC * CS].rearrange("p (c n) -> p c n", c=KC)[:, :, :N], xsrc[:, :, t0:t0 + N].rearrange("c p n -> p c n"), s_x,
                             waits=[xt_free] if not isinstance(xt_free, list) else xt_free)
                ev_res = []
                mm = None
                for oc in range(KC):
                    kw, wA, wfree = wA_r.next()
                    ev_w = P.dma("sp", wA, wo_b[l][:, oc * KC * 128:(oc + 1) * KC * 128], s_wA[kw], waits=[wfree, conv_ev[f"wo{l}"]])
                    kb_, bank, bfree = psR.next()
                    for kc in range(KC):
                        mm = P.op("pe", lambda e, kc=kc, wA=wA, bank=bank, mx=mx, N=N: e.matmul(bank[:, :N], lhsT=wA[:, kc * 128:(kc + 1) * 128], rhs=mx[:, kc * N:(kc + 1) * N],
                                                                                                 start=(kc == 0), stop=(kc == KC - 1)), sig=(kc == KC - 1), waits=[ev_w, ev_mx, bfree])
                    wA_r.free[kw] = mm
                    er = P.op("dve", lambda e, oc=oc, bank=bank, N=N, CS=CS: e.tensor_tensor(out=xt[:, oc * CS:oc * CS + N], in0=xt[:, oc * CS:oc * CS + N], in1=bank[:, :N], op=ALU.add),
                              sig=True, waits=[mm, ev_x])
                    psR.free[kb_] = er
                    ev_res.append(er)
                mx_r.free[km] = mm
                res["out_wait"] = [h2_free] if not isinstance(h2_free, list) else h2_free
                ev_h2 = rms(xt, N, L + l, lambda c, N=N: h2[:, c * N:(c + 1) * N], res, after=ev_res, CS=CS)
                ev_x2 = list(ev_res)
                last_up = None
                for half in range(2):
                    ev_a = []
                    for fl in range(FH):
                        fc = half * FH + fl
                        kw, wA, wfree = wA_r.next()
                        ev_w = P.dma("sp", wA, wup_b[l][:, fc * KC * 128:(fc + 1) * KC * 128], s_wA[kw], waits=[wfree, conv_ev[f"wup{l}"]])
                        kb_, bank, bfree = psR.next()
                        for kc in range(KC):
                            mm = P.op("pe", lambda e, kc=kc, wA=wA, bank=bank, N=N: e.matmul(bank[:, :N], lhsT=wA[:, kc * 128:(kc + 1) * 128], rhs=h2[:, kc * N:(kc + 1) * N],
                                                                                              start=(kc == 0), stop=(kc == KC - 1)), sig=(kc == KC - 1), waits=[ev_w, bfree] + ev_h2)
                        wA_r.free[kw] = mm
                        last_up = mm
                        kr, rst, rfree = rst_r.next()
                        e_r = P.op("act", lambda e, rst=rst, bank=bank, N=N: e.activation(out=rst[:, :N], in_=bank[:, :N], func=AF.Relu), sig=True, waits=[mm, rfree])
                        psR.free[kb_] = e_r
                        e_a = P.op("pool", lambda e, rst=rst, fl=fl, N=N: e.tensor_tensor(out=a_t[:, fl * N:(fl + 1) * N], in0=rst[:, :N], in1=rst[:, :N], op=ALU.mult),
                                   sig=True, waits=[e_r, a_free[0]])
                        rst_r.free[kr] = e_a
                        ev_a.append(e_a)
                    new_x2 = []
                    for oc in range(KC):
                        kw, wD, wfree = wD_r.next()
                        off = (half * KC + oc) * FH * 128
                        ev_w = P.dma("sp", wD, wdn_b[l][:, off:off + FH * 128], s_wD[kw], waits=[wfree, conv_ev[f"wdn{l}"]])
                        kb_, bank, bfree = psR.next()
                        for fl in range(FH):
                            mm = P.op("pe", lambda e, fl=fl, wD=wD, bank=bank, N=N: e.matmul(bank[:, :N], lhsT=wD[:, fl * 128:(fl + 1) * 128], rhs=a_t[:, fl * N:(fl + 1) * N],
                                                                                              start=(fl == 0), stop=(fl == FH - 1)), sig=(fl == FH - 1), waits=[ev_w, bfree] + ev_a)
                        wD_r.free[kw] = mm
                        er = P.op("dve", lambda e, oc=oc, bank=bank, N=N, CS=CS: e.tensor_tensor(out=xt[:, oc * CS:oc * CS + N], in0=xt[:, oc * CS:oc * CS + N], in1=bank[:, :N], op=ALU.add),
                                  sig=True, waits=[mm, ev_x2[oc]])
                        psR.free[kb_] = er
                        new_x2.append(er)
                    a_free[0] = mm
                    ev_x2 = new_x2
                if not last:
                    ev_st = P.dma("pool", xres[sk][:, :, t0:t0 + N].rearrange("c p n -> p c n"), xt[:, :KC * CS].rearrange("p (c n) -> p c n", c=KC)[:, :, :N], s_xr, waits=ev_x2)
                    res["out_wait"] = [last_up, h2_free] if not isinstance(h2_free, list) else [last_up] + h2_free
                    ev_hn = rms(xt, N, l + 1, lambda c, N=N: h2[:, c * N:(c + 1) * N], res, after=ev_x2, CS=CS)
                    ev_sh = P.dma("pool", hT[sk][:, :, t0:t0 + N].rearrange("c p n -> p c n"), h2[:, :KC * N].rearrange("p (c n) -> p c n", c=KC), s_hh, waits=ev_hn)
                    h2_free = [ev_sh]
                    xt_free = [ev_st, ev_hn[-1]]
                else:
                    h2_free = [last_up]
                    res["out_wait"] = []
                    ev_y = rms(xt, N, 2 * L, lambda c, N=N, CS=CS: xt[:, c * CS:c * CS + N], res, after=ev_x2, CS=CS)
                    last_tr = None
                    for tb in range((N + 127) // 128 if not cfg.get("noy") else 0):
                        nt = min(128, N - tb * 128)
                        for c4 in range(KC // 4):
                            kb_, bank, bfree = psY.next()
                            tr = None
                            for j in range(4):
                                c = c4 * 4 + j
                                tr = P.op("pe", lambda e, c=c, j=j, bank=bank, N=N, tb=tb, nt=nt, CS=CS: e.transpose(bank[:, j * 128:(j + 1) * 128], xt[:, c * CS + tb * 128:c * CS + tb * 128 + 128], CF("ident32")),
                                          sig=(j == 3), waits=[ev_y[c], bfree])
                            last_tr = tr
                            ky, yst, yfree = yst_r.next()
                            ec = P.op("dve", lambda e, yst=yst, bank=bank, nt=nt: e.tensor_copy(out=yst[:nt, :], in_=bank[:nt, :]), sig=True, waits=[tr, yfree])
                            psY.free[kb_] = ec
                            yst_r.free[ky] = P.dma("pool", y[sk][t0 + tb * 128:t0 + tb * 128 + nt, c4 * 512:(c4 + 1) * 512], yst[:nt, :], s_y[ky], waits=[ec])
                    xt_free = [last_tr]

        for l in range(L):
            if STOP in ("setup", "p0"):
                break
            phase1(l)
            barrier()
            if STOP == f"p1_{l}":
                break
            if l == 0:
                conv_layer(1)
                phase2_sb(l)
            else:
                phase2_diff(l)
            barrier()
            if STOP == f"p2_{l}":
                break
            phase3(l)
            barrier()
            if STOP == f"p3_{l}":
                break

        mx = P.check_deadlock()
        print("sync plan ok; max sem value", mx, "ops", {n: len(P.q[n].ops) for n in Prog.ENG}, flush=True)
        with nc.Block() as block:
            @block.sync
            def _(e):
                for f in P.q["sp"].ops:
                    f(e)

            @block.tensor
            def _(e):
                for f in P.q["pe"].ops:
                    f(e)

            @block.scalar
            def _(e):
                for f in P.q["act"].ops:
                    f(e)

            @block.vector
            def _(e):
                for f in P.q["dve"].ops:
                    f(e)

            @block.gpsimd
            def _(e):
                for f in P.q["pool"].ops:
                    f(e)
    return nc


def _wtile(w, ncol_tile):
    K, C = w.shape
    kc = K // 128
    oc = C // ncol_tile
    t = w.reshape(kc, 128, oc, ncol_tile).transpose(1, 2, 0, 3)
    return np.ascontiguousarray(t.reshape(128, oc * kc * ncol_tile))


def prep_core_inputs(cfg, inputs, b_prompt, b_sample, consts):
    S, PAST, NS, D, DFF, L = cfg["S"], cfg["PAST"], cfg["NS"], cfg["D"], cfg["DFF"], cfg["L"]
    KC = D // 128
    FC = DFF // 128
    FH = FC // 2
    cbf, cf32 = consts
    m = {}
    m["xT_p"] = np.ascontiguousarray(inputs["x_prompt"][b_prompt].T.reshape(KC, 128, S))
    m["xT_s"] = np.ascontiguousarray(inputs["x_sample"][b_sample].T.reshape(KC, 128, NS))
    m["ck"] = np.ascontiguousarray(inputs["cache_k"][:, b_sample])
    m["cv"] = np.ascontiguousarray(inputs["cache_v"][:, b_sample])
    gl = []
    for l in range(L):
        gl.append(inputs["norm_attn"][l])
    for l in range(L):
        gl.append(inputs["norm_mlp"][l])
    gl.append(inputs["norm_final"])
    m["gains"] = np.ascontiguousarray(np.stack([g.reshape(KC, 128).T for g in gl], axis=1).reshape(128, (2 * L + 1) * KC))
    m["subg"] = np.ascontiguousarray(inputs["subln_g"][0].reshape(2, 128).T)
    lamv = np.concatenate([inputs["lambda_q1"][0], inputs["lambda_k1"][0], inputs["lambda_q2"][0], inputs["lambda_k2"][0]])
    m["lam"] = np.ascontiguousarray(np.broadcast_to(lamv[None, :], (128, 512)))
    m["cbf"] = cbf
    m["cf32"] = cf32
    return m


def prep_weights(cfg, inputs):
    D, DFF, L = cfg["D"], cfg["DFF"], cfg["L"]
    KC = D // 128
    FC = DFF // 128
    FH = FC // 2
    w = {}
    for l in range(L):
        wq = inputs["w_qkv"][l]
        w[f"wqk{l}"] = _wtile(wq[:, 0:2 * D], 128)
        w[f"wkv{l}"] = _wtile(wq[:, D:3 * D], 512)
        w[f"wo{l}"] = _wtile(inputs["w_o"][l], 128)
        w[f"wup{l}"] = _wtile(inputs["w_up"][l], 128)
        wd = inputs["w_down"][l]
        t = wd.reshape(2, FH, 128, KC, 128).transpose(2, 0, 3, 1, 4)
        w[f"wdn{l}"] = np.ascontiguousarray(t.reshape(128, 2 * KC * FH * 128))
    return w


_CACHE = {}


def run(cfg, inputs, n_cores=8):
    inputs = {k: np.asarray(v) for k, v in inputs.items()}
    B = inputs["x_prompt"].shape[0]
    BS = inputs["x_sample"].shape[0]
    key = tuple(sorted(cfg.items()))
    if key not in _CACHE:
        _CACHE[key] = build(cfg)
    nc = _CACHE[key]
    cbf, _, cf32, _ = make_consts()
    w = prep_weights(cfg, inputs)
    in_maps = []
    for c in range(n_cores):
        m = prep_core_inputs(cfg, inputs, c % B, c % BS, (cbf, cf32))
        m.update(w)
        in_maps.append(m)
    res = run_bass_kernel_spmd(nc, in_maps, core_ids=list(range(n_cores)))
    r = res.results
    y_p = np.stack([r[b]["y_p"] for b in range(B)])
    y_s = np.stack([r[b]["y_s"] for b in range(BS)])
    nk_p = np.stack([r[b]["nk_p"] for b in range(B)], axis=1)
    nv_p = np.stack([r[b]["nv_p"] for b in range(B)], axis=1)
    nk_s = np.stack([r[b]["nk_s"] for b in range(BS)], axis=1)
    nv_s = np.stack([r[b]["nv_s"] for b in range(BS)], axis=1)
    return (y_p.astype(np.float32), y_s.astype(np.float32), nk_p.astype(np.float32), nv_p.astype(np.float32),
            nk_s.astype(np.float32), nv_s.astype(np.float32))


def kernel(**inputs):
    return run(FULL, inputs)
```
